# Optimizing a Trainium2 kernel written in Bass

```python
import math
import jax, jax.numpy as jnp
from jax import lax
import numpy as np

D_MODEL = 4096
BATCH = 8
SEQ = 2048
DEPTH = 2

CHUNK = 64
N_GROUPS = 4
GROUP_W = D_MODEL // N_GROUPS
D_MIX = N_GROUPS * GROUP_W
SGU_CHUNK = 128
SGU_HEADS = 8
CONV_WIDTH = 31
CONV_GROUPS = 8
RET_HEADS = 4
RET_HEAD_DIM = GROUP_W // RET_HEADS
MEM_LEN = 256
MEM_HEADS = 4
MEM_HEAD_DIM = GROUP_W // MEM_HEADS
ROPE_BASE = 10000.0
EPS = 1e-6
N_IN_SLICES = 12
W_IN_COLS = N_IN_SLICES * GROUP_W

kernel_name = "hybrid_sgu_conformer_retention_memory_block"


def rms_norm(x, g):
    xf = x.astype(jnp.float32)
    y = xf * lax.rsqrt(jnp.mean(xf * xf, axis=-1, keepdims=True) + EPS)
    return (y * g.astype(jnp.float32)).astype(x.dtype)


def group_layer_norm(x, n_groups, g, b):
    shp = x.shape
    xf = x.astype(jnp.float32).reshape(shp[:-1] + (n_groups, shp[-1] // n_groups))
    mu = jnp.mean(xf, axis=-1, keepdims=True)
    var = jnp.mean(jnp.square(xf - mu), axis=-1, keepdims=True)
    y = ((xf - mu) * lax.rsqrt(var + EPS)).reshape(shp)
    return (y * g.astype(jnp.float32) + b.astype(jnp.float32)).astype(x.dtype)


def sgu_spatial_mix(v, w_s, b_s):
    bsz, s_len, _ = v.shape
    n = s_len // SGU_CHUNK
    vb = v.reshape(bsz, n, SGU_CHUNK, SGU_HEADS, GROUP_W // SGU_HEADS)
    mask = jnp.tril(jnp.ones((SGU_CHUNK, SGU_CHUNK), v.dtype))
    out = jnp.einsum('gts,bnsgc->bntgc', w_s * mask, vb) + b_s.T[None, None, :, :, None]
    return out.reshape(bsz, s_len, GROUP_W)


def causal_depthwise_conv(x, w, b):
    y = lax.conv_general_dilated(
        x, w[:, None, :], window_strides=(1,), padding=[(CONV_WIDTH - 1, 0)],
        dimension_numbers=('NWC', 'WIO', 'NWC'), feature_group_count=x.shape[-1])
    return y + b


def rotary(x, cos, sin):
    x1, x2 = jnp.split(x, 2, axis=-1)
    return jnp.concatenate([x1 * cos - x2 * sin, x2 * cos + x1 * sin], axis=-1)


def retention_chunkwise(q, k, v):
    bsz, s_len, n_h, d = q.shape
    n = s_len // CHUNK
    dt = v.dtype
    log_g = jnp.log(1.0 - jnp.power(2.0, -5.0 - jnp.arange(n_h, dtype=jnp.float32)))
    t = jnp.arange(CHUNK, dtype=jnp.float32)
    diff = t[:, None] - t[None, :]
    intra_decay = jnp.where(diff >= 0, jnp.exp(log_g[:, None, None] * jnp.maximum(diff, 0.0)), 0.0).astype(dt)
    k_decay = jnp.exp(log_g[:, None] * (CHUNK - 1 - t)[None, :]).astype(dt)
    q_decay = jnp.exp(log_g[:, None] * (t + 1)[None, :]).astype(dt)
    chunk_decay = jnp.exp(log_g * CHUNK).astype(dt)
    qc = q.reshape(bsz, n, CHUNK, n_h, d)
    kc = k.reshape(bsz, n, CHUNK, n_h, d)
    vc = v.reshape(bsz, n, CHUNK, n_h, d)
    scores = jnp.einsum('bnthd,bnshd->bnhts', qc, kc) * intra_decay
    intra = jnp.einsum('bnhts,bnshe->bnthe', scores, vc)
    chunk_kv = jnp.einsum('bnshd,hs,bnshe->bnhde', kc, k_decay, vc)

    def step(state, kv):
        return chunk_decay[None, :, None, None] * state + kv, state

    _, prev = lax.scan(step, jnp.zeros((bsz, n_h, d, d), dt), jnp.moveaxis(chunk_kv, 1, 0))
    prev = jnp.moveaxis(prev, 0, 1)
    cross = jnp.einsum('bnthd,ht,bnhde->bnthe', qc, q_decay, prev)
    return (intra + cross).reshape(bsz, s_len, n_h, d)


def memory_attention(q, mem_n, w_mem_kv):
    bsz, m_len, _ = mem_n.shape
    kv = jnp.einsum('bmd,de->bme', mem_n, w_mem_kv)
    k_m, v_m = jnp.split(kv, 2, axis=-1)
    k_m = k_m.reshape(bsz, m_len, MEM_HEADS, MEM_HEAD_DIM)
    v_m = v_m.reshape(bsz, m_len, MEM_HEADS, MEM_HEAD_DIM)
    s = jnp.einsum('bshd,bmhd->bhsm', q, k_m).astype(jnp.float32) * (MEM_HEAD_DIM ** -0.5)
    p = jax.nn.softmax(s, axis=-1).astype(v_m.dtype)
    return jnp.einsum('bhsm,bmhd->bshd', p, v_m).reshape(q.shape[0], q.shape[1], GROUP_W)


def setup_inputs(seed: int = 0) -> dict:
    key = jax.random.key(seed)
    ks = jax.random.split(key, 24)
    f32 = jnp.float32
    nrm = lambda k, shp, sc: jax.random.normal(k, shp, f32) * sc
    x = nrm(ks[0], (BATCH, SEQ, D_MODEL), 1.0)
    mem = nrm(ks[1], (BATCH, MEM_LEN, D_MODEL), 1.0)
    start = jax.random.randint(ks[2], (BATCH, 1), 0, 4096, dtype=jnp.int32)
    positions = (start + jnp.arange(SEQ, dtype=jnp.int32)[None, :]).astype(jnp.int32)
    return {
        "x": x,
        "mem": mem,
        "positions": positions,
        "norm_g": 1.0 + nrm(ks[3], (DEPTH, D_MODEL), 0.02),
        "w_in": nrm(ks[4], (DEPTH, D_MODEL, W_IN_COLS), D_MODEL ** -0.5),
        "sgu_norm_g": 1.0 + nrm(ks[5], (DEPTH, GROUP_W), 0.02),
        "sgu_norm_b": nrm(ks[6], (DEPTH, GROUP_W), 0.02),
        "sgu_w": nrm(ks[7], (DEPTH, SGU_HEADS, SGU_CHUNK, SGU_CHUNK), 0.5 * SGU_CHUNK ** -0.5),
        "sgu_b": 1.0 + nrm(ks[8], (DEPTH, SGU_HEADS, SGU_CHUNK), 0.02),
        "conv_w": nrm(ks[9], (DEPTH, CONV_WIDTH, GROUP_W), CONV_WIDTH ** -0.5),
        "conv_b": nrm(ks[10], (DEPTH, GROUP_W), 0.02),
        "conv_norm_g": 1.0 + nrm(ks[11], (DEPTH, GROUP_W), 0.02),
        "conv_norm_b": nrm(ks[12], (DEPTH, GROUP_W), 0.02),
        "ret_norm_g": 1.0 + nrm(ks[13], (DEPTH, GROUP_W), 0.02),
        "ret_norm_b": nrm(ks[14], (DEPTH, GROUP_W), 0.02),
        "mem_norm_g": 1.0 + nrm(ks[15], (DEPTH, D_MODEL), 0.02),
        "w_mem_kv": nrm(ks[16], (DEPTH, D_MODEL, 2 * GROUP_W), D_MODEL ** -0.5),
        "w_out": nrm(ks[17], (DEPTH, D_MIX, D_MODEL), D_MIX ** -0.5),
        "final_norm_g": 1.0 + nrm(ks[18], (D_MODEL,), 0.02),
    }


def reference(x, mem, positions, norm_g, w_in, sgu_norm_g, sgu_norm_b, sgu_w, sgu_b,
              conv_w, conv_b, conv_norm_g, conv_norm_b, ret_norm_g, ret_norm_b,
              mem_norm_g, w_mem_kv, w_out, final_norm_g):
    bsz, s_len, _ = x.shape
    half = RET_HEAD_DIM // 2
    inv_freq = jnp.power(ROPE_BASE, -jnp.arange(half, dtype=jnp.float32) / half)
    ang = positions.astype(jnp.float32)[..., None] * inv_freq
    cos = jnp.cos(ang)[:, :, None, :].astype(x.dtype)
    sin = jnp.sin(ang)[:, :, None, :].astype(x.dtype)

    for l in range(DEPTH):
        h = rms_norm(x, norm_g[l])
        proj = jnp.einsum('bsd,de->bse', h, w_in[l])
        (a_u, a_v, a_g, b_a, b_b, b_g,
         c_q, c_k, c_v, c_g, m_q, m_g) = jnp.split(proj, N_IN_SLICES, axis=-1)

        a_v = group_layer_norm(a_v, 1, sgu_norm_g[l], sgu_norm_b[l])
        y_a = a_u * sgu_spatial_mix(a_v, sgu_w[l], sgu_b[l]) * jax.nn.silu(a_g)

        glu = b_a * jax.nn.sigmoid(b_b)
        cv = causal_depthwise_conv(glu, conv_w[l], conv_b[l])
        cv = group_layer_norm(cv, CONV_GROUPS, conv_norm_g[l], conv_norm_b[l])
        y_b = jax.nn.silu(cv) * jax.nn.silu(b_g)

        q = rotary(c_q.reshape(bsz, s_len, RET_HEADS, RET_HEAD_DIM), cos, sin)
        k = rotary(c_k.reshape(bsz, s_len, RET_HEADS, RET_HEAD_DIM), cos, sin) * (RET_HEAD_DIM ** -0.5)
        v = c_v.reshape(bsz, s_len, RET_HEADS, RET_HEAD_DIM)
        r = retention_chunkwise(q, k, v).reshape(bsz, s_len, GROUP_W)
        r = group_layer_norm(r, RET_HEADS, ret_norm_g[l], ret_norm_b[l])
        y_c = r * jax.nn.silu(c_g)

        mem_n = rms_norm(mem, mem_norm_g[l])
        mq = m_q.reshape(bsz, s_len, MEM_HEADS, MEM_HEAD_DIM)
        y_m = memory_attention(mq, mem_n, w_mem_kv[l]) * jax.nn.silu(m_g)

        y = jnp.concatenate([y_a, y_b, y_c, y_m], axis=-1)
        x = x + jnp.einsum('bse,ed->bsd', y, w_out[l])

    return rms_norm(x, final_norm_g)
```

```python
import contextlib
import math
import numpy as np
import ml_dtypes
import concourse.bass as bass
import concourse.mybir as mybir
from concourse.bass_utils import run_bass_kernel_spmd

dt = mybir.dt
AF = mybir.ActivationFunctionType
ALU = mybir.AluOpType
F32 = dt.float32
BF16 = dt.bfloat16
I32 = dt.int32

ENGS = ['pe', 'act', 'dve', 'pool', 'sp']


class Op:
    __slots__ = ('eng', 'fn', 'tl', 'tseq', 'waits', 'need_inc', 'cclock', 'is_dma')


class Prog:
    def __init__(self, self_sync=True):
        self.streams = {e: [] for e in ENGS}
        self.ops_by_tl = {}
        self.last_write = {}
        self.readers = {}
        self.eng_clock = {e: {} for e in ENGS}
        self.self_sync = self_sync

    def add(self, eng, fn, reads=(), writes=(), dma=None):
        deps = {}
        lw = self.last_write
        for r in reads:
            d = lw.get(r)
            if d is not None and deps.get(d[0], -1) < d[1]:
                deps[d[0]] = d[1]
            if type(r) is tuple and r[0] == 'ps':
                rd = self.readers.get(r)
                if rd:
                    for tl, ts in rd.items():
                        if tl != eng and deps.get(tl, -1) < ts:
                            deps[tl] = ts
        for w in writes:
            d = lw.get(w)
            if d is not None and deps.get(d[0], -1) < d[1]:
                deps[d[0]] = d[1]
            rd = self.readers.get(w)
            if rd:
                for tl, ts in rd.items():
                    if deps.get(tl, -1) < ts:
                        deps[tl] = ts
        clock = self.eng_clock[eng]
        waits = {}
        for tl, ts in deps.items():
            if clock.get(tl, -1) >= ts:
                continue
            if tl == eng and (eng == 'pe' or not self.self_sync):
                continue
            waits[tl] = ts
        for tl, ts in waits.items():
            dop = self.ops_by_tl[tl][ts]
            dop.need_inc = True
            for k, v in dop.cclock.items():
                if clock.get(k, -1) < v:
                    clock[k] = v
        op = Op()
        op.eng = eng
        op.fn = fn
        op.waits = waits
        op.need_inc = False
        op.is_dma = dma is not None
        tl = ('dma:' + str(dma)) if dma is not None else eng
        lst = self.ops_by_tl.setdefault(tl, [])
        op.tl = tl
        op.tseq = len(lst)
        lst.append(op)
        cc = dict(clock)
        cc[tl] = op.tseq
        op.cclock = cc
        self.streams[eng].append(op)
        for r in reads:
            self.readers.setdefault(r, {})[tl] = op.tseq
        me = (tl, op.tseq)
        for w in writes:
            lw[w] = me
            self.readers[w] = {}
        return op

    def emit(self, nc, stack):
        sems = {}
        tickets = {}
        for tl, lst in self.ops_by_tl.items():
            if not any(o.need_inc for o in lst):
                continue
            sems[tl] = stack.enter_context(nc.semaphore('s_' + tl.replace(':', '_')))
            c = 0
            tk = []
            step = 16 if tl.startswith('dma:') else 1
            for o in lst:
                if o.need_inc:
                    c += step
                tk.append(c)
            tickets[tl] = tk
        self.n_sems = len(sems)
        self.max_ticket = max([t[-1] for t in tickets.values()] + [0])

        def run(e, ops):
            for o in ops:
                for tl, ts in o.waits.items():
                    e.wait_ge(sems[tl], tickets[tl][ts])
                inst = o.fn(e)
                if o.need_inc:
                    inst.then_inc(sems[o.tl], 16 if o.is_dma else 1)

        block = stack.enter_context(nc.Block())
        S = self.streams

        @block.sync
        def _(e):
            run(e, S['sp'])

        @block.scalar
        def _(e):
            run(e, S['act'])

        @block.vector
        def _(e):
            run(e, S['dve'])

        @block.gpsimd
        def _(e):
            run(e, S['pool'])

        @block.tensor
        def _(e):
            run(e, S['pe'])


D = 4096
KT = 32
TB = 1024
NTT = 8
EPS = 1e-6
EARLY_NORM = True
PPL = 360
PP_NG, PP_MG, PP_SG, PP_SB, PP_CG, PP_CB, PP_RG, PP_RB, PP_CW = 0, 32, 64, 72, 80, 88, 96, 104, 112
PP_IF = 2 * PPL
PP_KD = 2 * PPL + 1
NPP = 2 * PPL + 5
CF_MASK, CF_QD, CF_TRIL = 0, 512, 1024
NCF = 1152
GAMMA = [1.0 - 2.0 ** (-5.0 - h) for h in range(4)]
TWO_PI = 2.0 * math.pi
CW1 = 6.28125
CW2 = TWO_PI - CW1


def _const_tables():
    cf = np.zeros((128, NCF), np.float32)
    s = np.arange(128)[:, None].astype(np.float64)
    t = np.arange(128)[None, :].astype(np.float64)
    for h in range(4):
        g = GAMMA[h]
        m = np.where(t >= s, g ** np.maximum(t - s, 0.0), 0.0) / 16.0
        cf[:, CF_MASK + h * 128:CF_MASK + (h + 1) * 128] = m
        cf[:, CF_QD + h * 128:CF_QD + (h + 1) * 128] = np.broadcast_to(g ** (t + 1.0), (128, 128))
    cf[:, CF_TRIL:CF_TRIL + 128] = (s <= t)
    cb = np.concatenate([np.eye(128), np.ones((128, 128))], axis=1).astype(ml_dtypes.bfloat16)
    return cf, cb


def _pp_table(inp, L):
    pp = np.zeros((128, NPP), np.float32)
    col = lambda v, n: np.asarray(v, np.float32).reshape(n, 128).T
    for l in range(L):
        b = l * PPL
        pp[:, b + PP_NG:b + PP_NG + 32] = col(inp["norm_g"][l], 32)
        pp[:, b + PP_MG:b + PP_MG + 32] = col(inp["mem_norm_g"][l], 32)
        pp[:, b + PP_SG:b + PP_SG + 8] = col(inp["sgu_norm_g"][l], 8)
        pp[:, b + PP_SB:b + PP_SB + 8] = col(inp["sgu_norm_b"][l], 8)
        pp[:, b + PP_CG:b + PP_CG + 8] = col(inp["conv_norm_g"][l], 8)
        pp[:, b + PP_CB:b + PP_CB + 8] = col(inp["conv_norm_b"][l], 8)
        pp[:, b + PP_RG:b + PP_RG + 8] = col(inp["ret_norm_g"][l], 8)
        pp[:, b + PP_RB:b + PP_RB + 8] = col(inp["ret_norm_b"][l], 8)
        cw = np.asarray(inp["conv_w"][l], np.float32).reshape(31, 8, 128)
        pp[:, b + PP_CW:b + PP_CW + 248] = cw.transpose(2, 1, 0).reshape(128, 248)
    i = np.arange(128, dtype=np.float64)
    pp[:, PP_IF] = (10000.0 ** (-i / 128.0)).astype(np.float32)
    sidx = np.arange(128, dtype=np.float64)
    for h in range(4):
        pp[:, PP_KD + h] = (GAMMA[h] ** (127.0 - sidx) / 16.0).astype(np.float32)
    return pp


class _Stop(Exception):
    pass


def build(S=2048, L=2, debug=False, stop_at=None):
    try:
        return _build(S, L, debug, stop_at)
    except _Stop as ex:
        return ex.args[0]


def _build(S, L, debug, stop_at):
    NTB = S // TB
    nc = bass.Bass("TRN2", target_bir_lowering=False)
    din = lambda n, s, d=F32: nc.dram_tensor(n, s, d, kind="ExternalInput").ap()
    x_d = din("x", [S, D]); mem_d = din("mem", [256, D]); pos_d = din("pos", [1, S], I32)
    win_d = din("w_in", [L, D, 12288]); wmkv_d = din("w_mkv", [L, D, 2048]); wout_d = din("w_out", [L, D, D])
    sguT_d = din("sguT", [L, 128, 1024]); sgub_d = din("sgub", [L, 1, 1024]); convb_d = din("convb", [L, 1, 1024])
    fng_d = din("fng", [1, D]); pp_d = din("pp", [128, NPP]); cf_d = din("cf", [128, NCF]); cb_d = din("cb", [128, 256], BF16)
    out_d = nc.dram_tensor("out", [S, D], F32, kind="ExternalOutput").ap()
    skind = "ExternalOutput" if debug else "Internal"
    xs_d = [x_d] + [nc.dram_tensor(f"xscr{i}", [S, D], F32, kind=skind).ap() for i in range(L)]
    yd_d = nc.dram_tensor("yscr", [32, 128, TB], BF16, kind=skind).ap()
    hd_d = nc.dram_tensor("hscr", [32, 128, TB], BF16, kind="Internal").ap()

    P = Prog()
    st = contextlib.ExitStack()
    with st:
        sbt = lambda n, s, d=F32: st.enter_context(nc.sbuf_tensor("sb_" + n, s, d))
        PS = st.enter_context(nc.psum_tensor("PS", [128, 8, 512], F32))
        PSB = PS.bitcast(BF16)
        actT = sbt("actT", [128, KT, TB], BF16)
        wb = [sbt(f"wb{i}", [128, KT, 256], BF16) for i in range(2)]
        pp = sbt("pp", [128, NPP]); cf = sbt("cf", [128, NCF]); cb = sbt("cb", [128, 256], BF16)
        kmT = sbt("kmT", [128, 8, 256], BF16); vm = sbt("vm", [128, 2, 1024], BF16)
        WmT = sbt("WmT", [128, 8, 128], BF16); Bias = sbt("Bias", [128, 8, 128])
        convb = sbt("convb", [1, 1024], BF16)
        cosT = sbt("cosT", [128, TB]); sinT = sbt("sinT", [128, TB])
        Sst = sbt("Sst", [128, 4, 2, 256]); Sall = sbt("Sall", [128, 8, 2, 256], BF16)
        gtail = sbt("gtail", [128, 8, 32], BF16)
        BIG = sbt("BIG", [128, D])
        vtok = BIG.bitcast(BF16)
        G = sbt("G", [128, 4, 1024])
        H = sbt("H", [128, 8, 1024], BF16)
        diag = sbt("diag", [128, 32, 128], BF16)
        gluE = sbt("gluE", [128, 1056], BF16)
        PTall = sbt("PTall", [128, 8, 128], BF16)
        st6 = sbt("st6", [128, 8, 6]); mv = sbt("mv", [128, 8, 2]); sm = sbt("sm", [128, 32])

        def mark(name):
            if stop_at == name:
                P.add('sp', lambda e: e.nop(), reads=list(P.last_write.keys()))
                P.emit(nc, st)
                build.info = dict(sems=P.n_sems, n={k: len(v) for k, v in P.streams.items()})
                raise _Stop(nc)

        ident = cb[:, 0:128]
        ones = cb[:, 128:256]
        wcnt = [0]

        def dma(eng, out, in_, reads, writes, key):
            return P.add(eng, lambda e: e.dma_start(out=out, in_=in_), reads=reads, writes=writes, dma=key)

        def act(out, in_, func, reads, writes, scale=None, bias=None, accum=None):
            kw = {}
            if scale is not None and not isinstance(scale, float) and func == AF.Copy:
                func = AF.Identity
            if scale is not None: kw['scale'] = scale
            if bias is not None: kw['bias'] = bias
            if accum is not None: kw['accum_out'] = accum
            return P.add('act', lambda e: e.activation(out=out, in_=in_, func=func, **kw), reads=reads, writes=writes)

        def tt(out, in0, in1, op, reads, writes, eng='dve'):
            return P.add(eng, lambda e: e.tensor_tensor(out=out, in0=in0, in1=in1, op=op), reads=reads, writes=writes)

        def ts(out, in0, s1, s2, op0, op1, reads, writes, eng='dve'):
            if op1 is None:
                return P.add(eng, lambda e: e.tensor_scalar(out=out, in0=in0, scalar1=s1, scalar2=None, op0=op0), reads=reads, writes=writes)
            return P.add(eng, lambda e: e.tensor_scalar(out=out, in0=in0, scalar1=s1, scalar2=s2, op0=op0, op1=op1), reads=reads, writes=writes)

        def stt(out, in0, scalar, in1, op0, op1, reads, writes):
            return P.add('dve', lambda e: e.scalar_tensor_tensor(out=out, in0=in0, scalar=scalar, in1=in1, op0=op0, op1=op1), reads=reads, writes=writes)

        def mm(out, lhsT, rhs, start, stop, reads, writes):
            return P.add('pe', lambda e: e.matmul(out, lhsT=lhsT, rhs=rhs, start=start, stop=stop), reads=reads, writes=writes)

        def tr(out, in_, reads, writes):
            return P.add('pe', lambda e: e.transpose(out, in_, ident), reads=list(reads) + ['cb'], writes=writes)

        def cpy(eng, out, in_, reads, writes):
            return P.add(eng, lambda e: e.tensor_copy(out=out, in_=in_), reads=reads, writes=writes)

        def memset(eng, ap, val, writes):
            return P.add(eng, lambda e: e.memset(ap, val), writes=writes)

        psk = lambda b: [('ps', b)]
        ACTK = [('act', k) for k in range(KT)]
        Gk = lambda *i: ['G%d' % j for j in i]
        Hk = lambda *i: ['H%d' % j for j in i]
        Gv = lambda i: G[:, i, :]
        Hv = lambda i: H[:, i, :]

        def load_w(src):
            b = wcnt[0] % 2
            wcnt[0] += 1
            dma('pool', wb[b][:, :, :], src.rearrange("(k p) c -> p k c", p=128), [], [('wb', b)], f'wb{b}')
            return b

        def rstd_from(var_ap, out_ap, n, keys_r, keys_w):
            ts(sm[:, 8:8 + n], var_ap, EPS, None, ALU.add, None, keys_r, ['sm8'])
            act(sm[:, 8:8 + n], sm[:, 8:8 + n], AF.Sqrt, ['sm8'], ['sm8'])
            P.add('dve', lambda e: e.reciprocal(out=out_ap, in_=sm[:, 8:8 + n]), reads=['sm8'], writes=keys_w)

        VTK = [('vt', t_) for t_ in range(NTT)]

        def norm_T(src_rows, gbase, tok0, src_keys, buf):
            if buf == 0:
                X = G[:, :, :].rearrange("p a b -> p (a b)"); Xk = Gk(0, 1, 2, 3)
                h0 = 0
            else:
                X = BIG[:, :]; Xk = ['BIG'] + VTK
                h0 = 4
            XS = H[:, h0:h0 + 4, :].rearrange("p a b -> p (a b)"); XSk = Hk(h0, h0 + 1, h0 + 2, h0 + 3)
            c0 = 16 + 4 * buf
            ssk, rsk = 'nss%d' % buf, 'nrs%d' % buf
            dma('sp', X, src_rows, src_keys, Xk, 'xt%d' % buf)
            act(XS, X, AF.Square, Xk, XSk + [ssk], accum=sm[:, c0:c0 + 1])
            ts(sm[:, c0 + 1:c0 + 2], sm[:, c0:c0 + 1], 1.0 / D, EPS, ALU.mult, ALU.add, [ssk], [rsk])
            act(sm[:, c0 + 1:c0 + 2], sm[:, c0 + 1:c0 + 2], AF.Sqrt, [rsk], [rsk])
            P.add('dve', lambda e: e.reciprocal(out=sm[:, c0 + 2:c0 + 3], in_=sm[:, c0 + 1:c0 + 2]), reads=[rsk], writes=[rsk + 'r'])
            act(XS, X, AF.Copy, Xk + [rsk + 'r'], XSk, scale=sm[:, c0 + 2:c0 + 3])
            for bi in range(4):
                bank = 4 + bi
                for j in range(8):
                    k = bi * 8 + j
                    tr(PSB[:, bank, j * 128:(j + 1) * 128], H[:, h0 + k // 8, (k % 8) * 128:(k % 8 + 1) * 128],
                       ['H%d' % (h0 + k // 8)], psk(bank))
                for j in range(8):
                    k = bi * 8 + j
                    dst = actT[:, k, tok0:tok0 + 128]
                    src = PSB[:, bank, j * 128:(j + 1) * 128]
                    gcol = pp[:, gbase + k:gbase + k + 1]
                    if bi % 2 == 0:
                        act(dst, src, AF.Copy, psk(bank) + ['pp'], [('act', k)], scale=gcol)
                    else:
                        ts(dst, src, gcol, None, ALU.mult, None, psk(bank) + ['pp'], [('act', k)])

        unit_par = [0]

        def fm_unit(b, u):
            p = unit_par[0] % 2
            unit_par[0] += 1
            for half in range(2):
                bank = 2 * p + half
                for k in range(KT):
                    mm(PS[:, bank, :], wb[b][:, k, u * 128:(u + 1) * 128], actT[:, k, half * 512:(half + 1) * 512],
                       k == 0, k == KT - 1, [('wb', b), ('act', k)], psk(bank))
            return p

        def fm_ps(p):
            return PS[:, 2 * p:2 * p + 2, :], psk(2 * p) + psk(2 * p + 1)

        v3 = lambda ap: ap.rearrange("p (a b) -> p a b", a=2)

        def tm_stage(b, dst_fn, dst_keys_fn):
            for t_ in range(NTT):
                bank, hf = t_ % 4, t_ // 4
                reg = PS[:, bank, hf * 256:(hf + 1) * 256]
                for k in range(KT):
                    mm(reg, actT[:, k, t_ * 128:(t_ + 1) * 128], wb[b][:, k, :], k == 0, k == KT - 1,
                       [('wb', b), ('act', k)], [('ps', bank)])
                act(dst_fn(t_), reg, AF.Copy, [('ps', bank)], dst_keys_fn(t_))

        ycnt = [0]

        def y_store(row, compute_fn):
            hb = ycnt[0] % 2
            ycnt[0] += 1
            compute_fn(Hv(hb), Hk(hb))
            dma('sp', yd_d[row, :, :], Hv(hb), Hk(hb), [('yd', row)], f'ys{hb}')

        DG = diag[:, :, :].rearrange("p a b -> p (a b)")
        STG = H[:, :, :].rearrange("p a b -> p (a b)").rearrange("p (k t) -> p k t", t=256)
        HALL = Hk(0, 1, 2, 3, 4, 5, 6, 7)
        outk = []
        early = {'normed': set(), 'final': set(), 'gfin': False}

        def side_norm_load(rows, src_keys):
            dma('sp', BIG[:, :], rows, src_keys, ['BIG'] + VTK, 'xt0')

        def side_norm_front():
            X = BIG[:, :]; Xk = ['BIG'] + VTK
            c0 = 16
            act(DG, X, AF.Square, Xk, ['diag', 'nss0'], accum=sm[:, c0:c0 + 1])
            ts(sm[:, c0 + 1:c0 + 2], sm[:, c0:c0 + 1], 1.0 / D, EPS, ALU.mult, ALU.add, ['nss0'], ['nrs0'])
            act(sm[:, c0 + 1:c0 + 2], sm[:, c0 + 1:c0 + 2], AF.Sqrt, ['nrs0'], ['nrs0'])
            P.add('dve', lambda e: e.reciprocal(out=sm[:, c0 + 2:c0 + 3], in_=sm[:, c0 + 1:c0 + 2]), reads=['nrs0'], writes=['nrs0r'])
            act(DG, X, AF.Copy, Xk + ['nrs0r'], ['diag'], scale=sm[:, c0 + 2:c0 + 3])

        def side_norm_store(i):
            for q in range(4):
                dma('sp', hd_d[q * 8:(q + 1) * 8, :, (i - 1) * 128:(i + 1) * 128].rearrange("k p t -> p k t"), STG[:, q * 8:(q + 1) * 8, :],
                    HALL, [('hd', i // 2, q)], 'hs%d' % q)

        def side_norm_back(i, gbase, store=True):
            for bi in range(4):
                bank = 4 + bi
                for j in range(8):
                    k = bi * 8 + j
                    tr(PSB[:, bank, j * 128:(j + 1) * 128], DG[:, k * 128:(k + 1) * 128], ['diag'], psk(bank))
                for j in range(8):
                    k = bi * 8 + j
                    act(STG[:, k, (i % 2) * 128:(i % 2 + 1) * 128], PSB[:, bank, j * 128:(j + 1) * 128], AF.Copy,
                        psk(bank) + ['pp'], HALL, scale=pp[:, gbase + k:gbase + k + 1])
            if store and i % 2 == 1:
                side_norm_store(i)

        XF = H.bitcast(F32)[:, :, :].rearrange("p a b -> p (a b)")

        def side_final_load(tt_):
            if not early['gfin']:
                dma('sp', BIG[:, :], fng_d.partition_broadcast(128), [], ['BIG'] + VTK, 'hs2')
                early['gfin'] = True
            dma('sp', XF, xs_d[L][tt_ * 128:(tt_ + 1) * 128, :], [('xd', L - 1, tt_ // NTT, tt_ % NTT, c) for c in range(16)], HALL, 'hs1')

        def side_final_compute(tt_):
            c0 = 20
            act(DG, XF, AF.Square, HALL, ['diag', 'nss1'], accum=sm[:, c0:c0 + 1])
            ts(sm[:, c0 + 1:c0 + 2], sm[:, c0:c0 + 1], 1.0 / D, EPS, ALU.mult, ALU.add, ['nss1'], ['nrs1'])
            act(sm[:, c0 + 1:c0 + 2], sm[:, c0 + 1:c0 + 2], AF.Sqrt, ['nrs1'], ['nrs1'])
            P.add('dve', lambda e: e.reciprocal(out=sm[:, c0 + 2:c0 + 3], in_=sm[:, c0 + 1:c0 + 2]), reads=['nrs1'], writes=['nrs1r'])
            stt(XF, XF, sm[:, c0 + 2:c0 + 3], BIG[:, :], ALU.mult, ALU.mult, HALL + ['nrs1r', 'BIG'], HALL)

        def side_final_store(tt_):
            dma('sp', out_d[tt_ * 128:(tt_ + 1) * 128, :], XF, HALL, [('out', tt_)], 'hs0')
            outk.append(('out', tt_))
            early['final'].add(tt_)

        dma('sp', pp[:, :], pp_d, [], ['pp'], 'c0')
        dma('sp', cf[:, :], cf_d, [], ['cf', 'pp'], 'c0')
        dma('sp', cb[:, :], cb_d, [], ['cb', 'cf', 'pp'], 'c0')

        def trig_tables(tb):
            posi = BIG.bitcast(I32)
            PK = ['BIG'] + VTK
            dma('sp', posi[:, 0:TB], pos_d[:, tb * TB:(tb + 1) * TB].partition_broadcast(128), [], PK, 'pos')
            cpy('dve', Gv(0), posi[:, 0:TB], PK, Gk(0))
            ts(Gv(0), Gv(0), pp[:, PP_IF:PP_IF + 1], None, ALU.mult, None, Gk(0) + ['pp'], Gk(0))
            ts(Gv(1), Gv(0), 1.0 / TWO_PI, None, ALU.mult, None, Gk(0), Gk(1))
            cpy('dve', posi[:, 0:TB], Gv(1), Gk(1), PK)
            cpy('dve', Gv(1), posi[:, 0:TB], PK, Gk(1))
            stt(Gv(0), Gv(1), -CW1, Gv(0), ALU.mult, ALU.add, Gk(0, 1), Gk(0))
            stt(Gv(0), Gv(1), -CW2, Gv(0), ALU.mult, ALU.add, Gk(0, 1), Gk(0))
            ts(Gv(2), Gv(0), math.pi, -math.pi, ALU.min, ALU.max, Gk(0), Gk(2))
            act(sinT[:, :], Gv(2), AF.Sin, Gk(2), ['sinT'])
            ts(Gv(1), Gv(0), math.pi / 2, None, ALU.add, None, Gk(0), Gk(1))
            ts(Gv(2), Gv(1), math.pi, None, ALU.is_gt, None, Gk(1), Gk(2))
            stt(Gv(1), Gv(2), -TWO_PI, Gv(1), ALU.mult, ALU.add, Gk(1, 2), Gk(1))
            ts(Gv(2), Gv(1), math.pi, -math.pi, ALU.min, ALU.max, Gk(1), Gk(2))
            act(cosT[:, :], Gv(2), AF.Sin, Gk(2), ['cosT'])

        mark('pro')
        for l in range(L):
            pb = l * PPL
            xsrc, xdst = xs_d[l], xs_d[l + 1]
            dma('sp', Gv(0), sguT_d[l], [], Gk(0), 'ls0')
            tt(WmT[:, :, :], Gv(0).rearrange("p (g t) -> p g t", g=8),
               cf[:, CF_TRIL:CF_TRIL + 128].unsqueeze(1).broadcast_to([128, 8, 128]), ALU.mult, Gk(0) + ['cf'], ['WmT'])
            dma('sp', Gv(1), sgub_d[l].partition_broadcast(128), [], Gk(1) + ['WmT'], 'ls0')
            for hh in range(2):
                mm(PS[:, 4 + hh, :], ones, WmT[:, hh * 4:(hh + 1) * 4, :].rearrange("p g t -> p (g t)"), True, True,
                   ['cb', 'WmT'], psk(4 + hh))
            for g in range(8):
                stt(Bias[:, g, :], PS[:, 4 + g // 4, (g % 4) * 128:(g % 4 + 1) * 128], pp[:, pb + PP_SB + g:pb + PP_SB + g + 1],
                    Gv(1)[:, g * 128:(g + 1) * 128], ALU.mult, ALU.add, psk(4 + g // 4) + Gk(1) + ['pp'], ['Bias'])
            dma('pool', convb[:, :], convb_d[l], [], ['convb'], 'ls2')
            memset('dve', Sst[:, :, :, :], 0.0, ['Sst'])
            memset('dve', gtail[:, :, :], 0.0, ['gtail'])
            mark('setup')
            for mt in range(2):
                norm_T(mem_d[mt * 128:(mt + 1) * 128, :], pb + PP_MG, mt * 128, [], mt)

            def memkv_stage(s_):
                b = load_w(wmkv_d[l][:, s_ * 256:(s_ + 1) * 256])
                if s_ < 4:
                    for u in range(2):
                        ct = s_ * 2 + u
                        bank = ct % 4
                        reg = PS[:, bank, 0:256]
                        for k in range(KT):
                            mm(reg, wb[b][:, k, u * 128:(u + 1) * 128], actT[:, k, 0:256], k == 0, k == KT - 1,
                               [('wb', b), ('act', k)], [('ps', bank)])
                        act(kmT[:, ct, :], reg, AF.Copy, [('ps', bank)], ['kmT'])
                else:
                    for mt in range(2):
                        i_ = (s_ - 4) * 2 + mt
                        bank = i_ % 4
                        reg = PS[:, bank, 0:256]
                        for k in range(KT):
                            mm(reg, actT[:, k, mt * 128:(mt + 1) * 128], wb[b][:, k, :], k == 0, k == KT - 1,
                               [('wb', b), ('act', k)], [('ps', bank)])
                        act(vm[:, mt, (s_ - 4) * 256:(s_ - 3) * 256], reg, AF.Copy, [('ps', bank)], ['vm'])

            for tb in range(NTB):
                t0 = tb * TB
                if (l, tb) in early['normed']:
                    if tb == 0:
                        for s_ in range(8):
                            memkv_stage(s_)
                    for q in range(4):
                        dma('sp', actT[:, q * 8:(q + 1) * 8, :], hd_d[q * 8:(q + 1) * 8, :, :].rearrange("k p t -> p k t"),
                            [('hd', j, q) for j in range(4)], [('act', k) for k in range(q * 8, (q + 1) * 8)], f'yl{q}')
                else:
                    order = [2, 3, 4, 5, 6, 7, 0, 1] if tb == 0 else list(range(NTT))
                    for i_, t_ in enumerate(order):
                        if tb == 0 and i_ == 6:
                            memkv_stage(6)
                            memkv_stage(7)
                        rows = xsrc[t0 + t_ * 128:t0 + (t_ + 1) * 128, :]
                        norm_T(rows, pb + PP_NG, t_ * 128, [('xd', l - 1, tb, t_, c) for c in range(16)], i_ % 2)
                        if tb == 0 and i_ < 6:
                            memkv_stage(i_)
                mark('normT')
                trig_tables(tb)

                mark('trig')
                wl = win_d[l]
                pending = []

                def flush():
                    nxt = []
                    while pending:
                        r_ = pending.pop(0)()
                        if callable(r_):
                            nxt.append(r_)
                    pending.extend(nxt)

                def drain():
                    while pending:
                        flush()

                def stage(fn_mm, fn_post):
                    fn_mm()
                    flush()
                    pending.append(fn_post)

                def fm_stage(col0, posts):
                    holder = {}
                    for u in range(2):
                        def fn_mm(u=u, holder=holder):
                            if u == 0:
                                holder['b'] = load_w(wl[:, col0:col0 + 256])
                            holder[u] = fm_unit(holder['b'], u)

                        def fn_post(u=u, holder=holder):
                            return posts[u](fm_ps(holder[u]))
                        stage(fn_mm, fn_post)

                def tm_group(col0):
                    drain()
                    for s_ in range(4):
                        b = load_w(wl[:, col0 + s_ * 256:col0 + (s_ + 1) * 256])
                        tm_stage(b, lambda t_, s_=s_: vtok[:, t_ * 1024 + s_ * 256:t_ * 1024 + (s_ + 1) * 256],
                                 lambda t_: [('vt', t_)])

                tm_group(1024)
                for t_ in range(NTT):
                    vrow = vtok[:, t_ * 1024:(t_ + 1) * 1024]
                    for j in range(2):
                        P.add('dve', lambda e, j=j, vrow=vrow: e.bn_stats(out=st6[:, j, :], in_=vrow[:, j * 512:(j + 1) * 512]),
                              reads=[('vt', t_)], writes=['st6'])
                    P.add('dve', lambda e: e.bn_aggr(out=mv[:, 0, :], in_=st6[:, 0:2, :]), reads=['st6'], writes=['mv'])
                    rstd_from(mv[:, 0, 1:2], sm[:, 3:4], 1, ['mv'], ['sm3'])
                    ts(vrow, vrow, mv[:, 0, 0:1], sm[:, 3:4], ALU.subtract, ALU.mult, [('vt', t_), 'mv', 'sm3'], [('vt', t_)])

                def a_post_gate(g):
                    gi = 0 if g % 2 == 0 else 3
                    return lambda ps: act(v3(Gv(gi)), ps[0], AF.Silu, ps[1], Gk(gi))

                def a_post_u(g):
                    u = g % 2
                    gi = 0 if u == 0 else 3
                    zi = 1 if u == 0 else 2

                    def f(ps):
                        tt(v3(Gv(gi)), ps[0], v3(Gv(gi)), ALU.mult, ps[1] + Gk(gi), Gk(gi))
                        for n in range(NTT):
                            bank = 4 + (n // 4) + 2 * u
                            mm(PS[:, bank, (n % 4) * 128:(n % 4 + 1) * 128],
                               vtok[:, n * 1024 + g * 128:n * 1024 + (g + 1) * 128], WmT[:, g, :], True, True,
                               [('vt', n), 'WmT'], psk(bank))
                        mixps = PS[:, 4 + 2 * u:6 + 2 * u, :].rearrange("p a (n t) -> p (a n) t", t=128)
                        stt(Gv(zi).rearrange("p (n t) -> p n t", t=128), mixps, pp[:, pb + PP_SG + g:pb + PP_SG + g + 1],
                            Bias[:, g, :].unsqueeze(1).broadcast_to([128, 8, 128]), ALU.mult, ALU.add,
                            psk(4 + 2 * u) + psk(5 + 2 * u) + ['pp', 'Bias'], Gk(zi))
                        y_store(g, lambda dst, dk: tt(dst, Gv(zi), Gv(gi), ALU.mult, Gk(zi, gi), dk))
                    return f

                for sp_ in range(4):
                    fm_stage(2048 + sp_ * 256, [a_post_gate(2 * sp_), a_post_gate(2 * sp_ + 1)])
                    fm_stage(sp_ * 256, [a_post_u(2 * sp_), a_post_u(2 * sp_ + 1)])

                mark('A')
                def b_post_b(ct):
                    gi = 0 if ct % 2 == 0 else 3
                    return lambda ps: act(v3(Gv(gi)), ps[0], AF.Sigmoid, ps[1], Gk(gi))

                def b_post_a(ct):
                    u = ct % 2
                    gi = 0 if u == 0 else 3
                    zi = 1 if u == 0 else 2
                    gk = 'gluE'

                    def f(ps):
                        cpy('dve', gluE[:, 0:30], gtail[:, ct, 0:30], ['gtail'], [gk])
                        tt(gluE[:, 30:30 + TB].rearrange("p (a b) -> p a b", a=2), ps[0], v3(Gv(gi)), ALU.mult,
                           ps[1] + Gk(gi), [gk])
                        cpy('dve', gtail[:, ct, 0:30], gluE[:, TB:TB + 30], [gk], ['gtail'])
                        wc = pb + PP_CW + ct * 31
                        diag2 = Sall[:, :, :, :].rearrange("p a b c -> p (a b c)")[:, 0:3968].rearrange("p (j c) -> p j c", c=128)
                        dg = diag[:, 0:31, :] if u == 0 else diag2
                        dk = 'diag' if u == 0 else 'Sall'
                        tt(dg, ident.unsqueeze(1).broadcast_to([128, 31, 128]),
                           pp[:, wc:wc + 31].unsqueeze(2).broadcast_to([128, 31, 128]), ALU.mult, ['cb', 'pp'], [dk])
                        for n in range(NTT):
                            bank = 4 + n // 4
                            reg = PS[:, bank, (n % 4) * 128:(n % 4 + 1) * 128]
                            for j in range(31):
                                mm(reg, gluE[:, n * 128 + j:n * 128 + j + 128], dg[:, j, :], j == 0, False,
                                   [gk, dk], psk(bank))
                            mm(reg, ones[0:1, :], convb[0:1, ct * 128:(ct + 1) * 128], False, True, ['cb', 'convb'], psk(bank))
                        for n in range(NTT):
                            bank = 4 + n // 4
                            reg = PS[:, bank, (n % 4) * 128:(n % 4 + 1) * 128]
                            P.add('dve', lambda e, n=n, reg=reg: e.bn_stats(out=st6[:, n, :], in_=reg), reads=psk(bank), writes=['st6'])
                        for n in range(NTT):
                            P.add('dve', lambda e, n=n: e.bn_aggr(out=mv[:, n, :], in_=st6[:, n:n + 1, :]), reads=['st6'], writes=['mv'])
                        rstd_from(mv[:, :, 1], sm[:, 0:8], 8, ['mv'], ['sm0'])
                        for n in range(NTT):
                            bank = 4 + n // 4
                            reg = PS[:, bank, (n % 4) * 128:(n % 4 + 1) * 128]
                            ts(H[:, 2, n * 128:(n + 1) * 128], reg, mv[:, n, 0:1], sm[:, n:n + 1], ALU.subtract, ALU.mult,
                               psk(bank) + ['mv', 'sm0'], Hk(2))
                        def part2():
                            for n in range(NTT):
                                tr(PSB[:, 6, n * 128:(n + 1) * 128], H[:, 2, n * 128:(n + 1) * 128], Hk(2), psk(6))
                            act(Gv(zi), PSB[:, 6, :], AF.Silu, psk(6) + ['pp'], Gk(zi),
                                scale=pp[:, pb + PP_CG + ct:pb + PP_CG + ct + 1], bias=pp[:, pb + PP_CB + ct:pb + PP_CB + ct + 1])
                        return part2
                    return f

                def b_post_g(ct):
                    u = ct % 2
                    gi = 0 if u == 0 else 3
                    zi = 1 if u == 0 else 2

                    def f(ps):
                        act(v3(Gv(gi)), ps[0], AF.Silu, ps[1], Gk(gi))
                        y_store(8 + ct, lambda dst, dk: tt(dst, Gv(zi), Gv(gi), ALU.mult, Gk(zi, gi), dk))
                    return f

                for sp_ in range(4):
                    fm_stage(4096 + sp_ * 256, [b_post_b(2 * sp_), b_post_b(2 * sp_ + 1)])
                    fm_stage(3072 + sp_ * 256, [b_post_a(2 * sp_), b_post_a(2 * sp_ + 1)])
                    fm_stage(5120 + sp_ * 256, [b_post_g(2 * sp_), b_post_g(2 * sp_ + 1)])

                mark('B')
                tm_group(8192)
                cos3, sin3 = v3(cosT[:, :]), v3(sinT[:, :])

                def rot_lo(ps):
                    act(v3(Gv(0)), ps[0], AF.Copy, ps[1], Gk(0))

                def rot_hi(ps, dst_i):
                    hi_ap, hi_k = ps
                    tt(v3(Gv(1)), hi_ap, sin3, ALU.mult, hi_k + ['sinT'], Gk(1))
                    tt(Gv(2), Gv(0), cosT[:, :], ALU.mult, Gk(0) + ['cosT'], Gk(2))
                    tt(Hv(dst_i), Gv(2), Gv(1), ALU.subtract, Gk(1, 2), Hk(dst_i))
                    tt(v3(Gv(1)), hi_ap, cos3, ALU.mult, hi_k + ['cosT'], Gk(1))
                    tt(Gv(2), Gv(0), sinT[:, :], ALU.mult, Gk(0) + ['sinT'], Gk(2))
                    tt(Hv(dst_i + 1), Gv(1), Gv(2), ALU.add, Gk(1, 2), Hk(dst_i + 1))

                def c_post_qhi(h):
                    def f(ps):
                        rot_hi(ps, 2)
                        for d_ in range(2):
                            tt(Hv(6 + d_).rearrange("p (n t) -> p n t", t=128), Hv(2 + d_).rearrange("p (n t) -> p n t", t=128),
                               cf[:, CF_QD + h * 128:CF_QD + (h + 1) * 128].unsqueeze(1).broadcast_to([128, 8, 128]), ALU.mult,
                               Hk(2 + d_) + ['cf'], Hk(6 + d_))
                    return f

                def c_post_khi(h):
                    def f(ps):
                        rot_hi(ps, 4)
                        ktok = diag[:, :, :].rearrange("p a b -> p (a b)")[:, 0:2048]
                        for n in range(NTT):
                            bank = 4 + n // 4
                            for d_ in range(2):
                                tr(PSB[:, bank, (n % 4) * 256 + d_ * 128:(n % 4) * 256 + (d_ + 1) * 128],
                                   H[:, 4 + d_, n * 128:(n + 1) * 128], Hk(4 + d_), psk(bank))
                        for bi in range(2):
                            act(ktok[:, bi * 1024:(bi + 1) * 1024], PSB[:, 4 + bi, :], AF.Copy, psk(4 + bi) + ['pp'], ['diag'],
                                scale=pp[:, PP_KD + h:PP_KD + h + 1])
                        ro = G[:, 2:4, :].rearrange("p a b -> p (a b)")
                        for n in range(NTT):
                            tk = slice(n * 128, (n + 1) * 128)
                            bank = 6 + n % 2
                            for d_ in range(2):
                                mm(PS[:, bank, 0:128], H[:, 4 + d_, tk], H[:, 2 + d_, tk], d_ == 0, d_ == 1, Hk(4 + d_, 2 + d_), psk(bank))
                            tt(PTall[:, n, :], PS[:, bank, 0:128], cf[:, CF_MASK + h * 128:CF_MASK + (h + 1) * 128], ALU.mult,
                               psk(bank) + ['cf'], ['PT'])
                        act(Sall[:, 0, :, :], Sst[:, h, :, :], AF.Copy, ['Sst'], ['Sall'])
                        for n in range(NTT):
                            bank = 4 + n % 2
                            vch = vtok[:, n * 1024 + h * 256:n * 1024 + (h + 1) * 256]
                            for d_ in range(2):
                                mm(PS[:, bank, d_ * 256:(d_ + 1) * 256], ktok[:, n * 256 + d_ * 128:n * 256 + (d_ + 1) * 128], vch, True, True,
                                   ['diag', ('vt', n)], psk(bank))
                            stt(Sst[:, h, :, :], Sst[:, h, :, :], GAMMA[h] ** 128, PS[:, bank, :].rearrange("p (d e) -> p d e", d=2),
                                ALU.mult, ALU.add, ['Sst'] + psk(bank), ['Sst'])
                            if n < NTT - 1:
                                act(Sall[:, n + 1, :, :], Sst[:, h, :, :], AF.Copy, ['Sst'], ['Sall'])
                        for n in range(NTT):
                            tk = slice(n * 128, (n + 1) * 128)
                            bank = 6 + n % 2
                            vch = vtok[:, n * 1024 + h * 256:n * 1024 + (h + 1) * 256]
                            mm(PS[:, bank, 0:256], PTall[:, n, :], vch, True, False, ['PT', ('vt', n)], psk(bank))
                            for d_ in range(2):
                                mm(PS[:, bank, 0:256], H[:, 6 + d_, tk], Sall[:, n, d_, :], False, d_ == 1, Hk(6 + d_) + ['Sall'], psk(bank))
                            act(ro[:, n * 256:(n + 1) * 256], PS[:, bank, 0:256], AF.Copy, psk(bank), Gk(2, 3))
                        for n in range(NTT):
                            P.add('dve', lambda e, n=n: e.bn_stats(out=st6[:, n, :], in_=ro[:, n * 256:(n + 1) * 256]), reads=Gk(2, 3), writes=['st6'])
                        for n in range(NTT):
                            P.add('dve', lambda e, n=n: e.bn_aggr(out=mv[:, n, :], in_=st6[:, n:n + 1, :]), reads=['st6'], writes=['mv'])
                        rstd_from(mv[:, :, 1], sm[:, 0:8], 8, ['mv'], ['sm0'])
                        rn = H[:, 4:6, :].rearrange("p a b -> p (a b)")
                        for n in range(NTT):
                            ts(rn[:, n * 256:(n + 1) * 256], ro[:, n * 256:(n + 1) * 256], mv[:, n, 0:1], sm[:, n:n + 1],
                               ALU.subtract, ALU.mult, Gk(2, 3) + ['mv', 'sm0'], Hk(4, 5))

                        def part2():
                            for e_ in range(2):
                                for n in range(NTT):
                                    tr(PSB[:, 4 + e_, n * 128:(n + 1) * 128], rn[:, n * 256 + e_ * 128:n * 256 + (e_ + 1) * 128],
                                       Hk(4, 5), psk(4 + e_))
                            for e_ in range(2):
                                c_ = h * 2 + e_
                                act(Gv(e_), PSB[:, 4 + e_, :], AF.Identity, psk(4 + e_) + ['pp'], Gk(e_),
                                    scale=pp[:, pb + PP_RG + c_:pb + PP_RG + c_ + 1], bias=pp[:, pb + PP_RB + c_:pb + PP_RB + c_ + 1])
                        return part2
                    return f

                def c_post_g(h, e_):
                    def f(ps):
                        act(v3(Gv(2 + e_)), ps[0], AF.Silu, ps[1], Gk(2 + e_))
                        y_store(16 + h * 2 + e_, lambda dst, dk: tt(dst, Gv(e_), Gv(2 + e_), ALU.mult, Gk(e_, 2 + e_), dk))
                    return f

                for h in range(4):
                    fm_stage(6144 + h * 256, [rot_lo, c_post_qhi(h)])
                    fm_stage(7168 + h * 256, [rot_lo, c_post_khi(h)])
                    fm_stage(9216 + h * 256, [c_post_g(h, 0), c_post_g(h, 1)])

                mark('C')
                def m_post_q(h, d_):
                    def f(ps):
                        act(v3(Hv(2 + d_)), ps[0], AF.Copy, ps[1], Hk(2 + d_))
                        if d_ == 0:
                            return
                        for hf in range(2):
                            tk = slice(hf * 512, (hf + 1) * 512)
                            for mt in range(2):
                                for dd in range(2):
                                    mm(PS[:, 4 + mt, :], kmT[:, h * 2 + dd, mt * 128:(mt + 1) * 128], H[:, 2 + dd, tk], dd == 0, dd == 1,
                                       ['kmT'] + Hk(2 + dd), psk(4 + mt))
                            pe_ = H[:, 4, :].rearrange("p (a b) -> p a b", a=2)
                            act(pe_, PS[:, 4:6, :], AF.Exp, psk(4) + psk(5), Hk(4), scale=1.0 / 16.0)
                            for e_ in range(2):
                                for mt in range(2):
                                    mm(PS[:, 6 + e_, :], vm[:, mt, h * 256 + e_ * 128:h * 256 + (e_ + 1) * 128], pe_[:, mt, :], mt == 0, mt == 1,
                                       ['vm'] + Hk(4), psk(6 + e_))
                            for mt in range(2):
                                mm(PS[:, 4, :], ones, pe_[:, mt, :], mt == 0, mt == 1, ['cb'] + Hk(4), psk(4))
                            P.add('dve', lambda e: e.reciprocal(out=G[:, 2, 0:512], in_=PS[:, 4, :]), reads=psk(4), writes=Gk(2))
                            for e_ in range(2):
                                tt(G[:, e_, tk], PS[:, 6 + e_, :], G[:, 2, 0:512], ALU.mult, psk(6 + e_) + Gk(2), Gk(e_))
                    return f

                def m_post_g(h, e_):
                    def f(ps):
                        act(v3(Gv(3)), ps[0], AF.Silu, ps[1], Gk(3))
                        y_store(24 + h * 2 + e_, lambda dst, dk: tt(dst, Gv(e_), Gv(3), ALU.mult, Gk(e_, 3), dk))
                    return f

                for h in range(4):
                    fm_stage(10240 + h * 256, [m_post_q(h, 0), m_post_q(h, 1)])
                    fm_stage(11264 + h * 256, [m_post_g(h, 0), m_post_g(h, 1)])
                drain()

                mark('M')
                for q in range(4):
                    dma('sp', actT[:, q * 8:(q + 1) * 8, :], yd_d[q * 8:(q + 1) * 8, :, :].rearrange("k p t -> p k t"),
                        [('yd', k) for k in range(q * 8, (q + 1) * 8)], [('act', k) for k in range(q * 8, (q + 1) * 8)], f'yl{q}')
                xsv = xsrc[t0:t0 + TB, :].rearrange("(t p) c -> p t c", p=128)
                xdv = xdst[t0:t0 + TB, :].rearrange("(t p) c -> p t c", p=128)
                if tb + 1 < NTB:
                    nxt = (l, tb + 1)
                elif l + 1 < L:
                    nxt = (l + 1, 0)
                else:
                    nxt = None
                if nxt is not None and EARLY_NORM:
                    early['normed'].add(nxt)
                fin_tiles = list(range((NTB - 1) * NTT)) if (nxt is None and EARLY_NORM) else []
                def side_step(c):
                    if nxt is not None and EARLY_NORM:
                        nl, ntb = nxt
                        if c % 2 == 0:
                            if c >= 2:
                                side_norm_back(c // 2 - 1, nl * PPL + PP_NG, store=False)
                            if c // 2 < NTT:
                                i_ = c // 2
                                r0 = ntb * TB + i_ * 128
                                side_norm_load(xs_d[nl][r0:r0 + 128, :], [('xd', nl - 1, ntb, i_, cc) for cc in range(16)])
                        else:
                            if c >= 3 and (c // 2 - 1) % 2 == 1:
                                side_norm_store(c // 2 - 1)
                            if c // 2 < NTT:
                                side_norm_front()
                    elif fin_tiles:
                        if c % 2 == 0:
                            if c >= 2 and c // 2 - 1 < len(fin_tiles):
                                side_final_store(fin_tiles[c // 2 - 1])
                            if c // 2 < len(fin_tiles):
                                side_final_load(fin_tiles[c // 2])
                        elif c // 2 < len(fin_tiles):
                            side_final_compute(fin_tiles[c // 2])

                for c in range(16):
                    b = load_w(wout_d[l][:, c * 256:(c + 1) * 256])
                    for tg in range(2):
                        gi = (c * 2 + tg) % 2
                        xin = Gv(gi).rearrange("p (t c) -> p t c", c=256)
                        dma('act', xin, xsv[:, tg * 4:(tg + 1) * 4, c * 256:(c + 1) * 256],
                            [('xd', l - 1, tb, tg * 4 + j, c) for j in range(4)], Gk(gi), f'xi{gi}')
                    side_step(c)
                    for tg in range(2):
                        gi = (c * 2 + tg) % 2
                        xin = Gv(gi).rearrange("p (t c) -> p t c", c=256)
                        xo = Gv(2 + gi).rearrange("p (t c) -> p t c", c=256)
                        for j in range(4):
                            t_ = tg * 4 + j
                            bank, hf = t_ % 4, t_ // 4
                            reg = PS[:, bank, hf * 256:(hf + 1) * 256]
                            for k in range(KT):
                                mm(reg, actT[:, k, t_ * 128:(t_ + 1) * 128], wb[b][:, k, :], k == 0, k == KT - 1,
                                   [('wb', b), ('act', k)], [('ps', bank)])
                            tt(xo[:, j, :], reg, xin[:, j, :], ALU.add, [('ps', bank)] + Gk(gi), Gk(2 + gi))
                        dma('sp', xdv[:, tg * 4:(tg + 1) * 4, c * 256:(c + 1) * 256], xo, Gk(2 + gi),
                            [('xd', l, tb, tg * 4 + j, c) for j in range(4)], f'xo{gi}')

                side_step(16)
                side_step(17)

        mark('p2')
        if not early['gfin']:
            dma('sp', BIG[:, :], fng_d.partition_broadcast(128), [], ['BIG'] + VTK, 'hs2')
        AF32 = actT.bitcast(F32)
        Gall = G[:, :, :].rearrange("p a b -> p (a b)")
        for tt_ in range(S // 128):
            if tt_ in early['final']:
                continue
            tb, t_ = tt_ // NTT, tt_ % NTT
            buf = tt_ % 4
            X = AF32[:, 8 * buf:8 * buf + 8, :].rearrange("p a b -> p (a b)")
            Xk = [('act', 8 * buf + j) for j in range(8)]
            c0 = 16 + 4 * buf
            ssk, rsk = 'nss%d' % buf, 'nrs%d' % buf
            dma('sp', X, xs_d[L][tt_ * 128:(tt_ + 1) * 128, :], [('xd', L - 1, tb, t_, c) for c in range(16)], Xk, ['xt0', 'xt1', 'xi0', 'xi1'][buf])
            act(Gall, X, AF.Square, Xk, Gk(0, 1, 2, 3) + [ssk], accum=sm[:, c0:c0 + 1])
            ts(sm[:, c0 + 1:c0 + 2], sm[:, c0:c0 + 1], 1.0 / D, EPS, ALU.mult, ALU.add, [ssk], [rsk])
            act(sm[:, c0 + 1:c0 + 2], sm[:, c0 + 1:c0 + 2], AF.Sqrt, [rsk], [rsk])
            P.add('dve', lambda e, c0=c0: e.reciprocal(out=sm[:, c0 + 2:c0 + 3], in_=sm[:, c0 + 1:c0 + 2]), reads=[rsk], writes=[rsk + 'r'])
            stt(X, X, sm[:, c0 + 2:c0 + 3], BIG[:, :], ALU.mult, ALU.mult, Xk + [rsk + 'r', 'BIG'], Xk)
            dma('act', out_d[tt_ * 128:(tt_ + 1) * 128, :], X, Xk, [('out', tt_)], ['xo0', 'xo1', 'ys0', 'ys1'][buf])
            outk.append(('out', tt_))
        P.add('sp', lambda e: e.nop(), reads=outk)
        P.emit(nc, st)
    build.info = dict(sems=P.n_sems, max_ticket=P.max_ticket, n={k: len(v) for k, v in P.streams.items()})
    return nc


def make_in_maps(inp, cores, S, L):
    cf, cb = _const_tables()
    pp = _pp_table(inp, L)
    f = lambda a: np.ascontiguousarray(np.asarray(a, np.float32))
    shared = dict(
        w_in=f(inp["w_in"][:L]), w_mkv=f(inp["w_mem_kv"][:L]), w_out=f(inp["w_out"][:L]),
        sguT=f(np.asarray(inp["sgu_w"][:L]).transpose(0, 3, 1, 2).reshape(L, 128, 1024)),
        sgub=f(np.asarray(inp["sgu_b"][:L]).reshape(L, 1, 1024)),
        convb=f(np.asarray(inp["conv_b"][:L]).reshape(L, 1, 1024)),
        fng=f(np.asarray(inp["final_norm_g"]).reshape(1, D)), pp=pp, cf=cf, cb=cb)
    maps = []
    for c in cores:
        m = dict(shared)
        m["x"] = f(inp["x"][c][:S])
        m["mem"] = f(inp["mem"][c])
        m["pos"] = np.ascontiguousarray(np.asarray(inp["positions"][c][:S], np.int32).reshape(1, S))
        maps.append(m)
    return maps


def kernel(**inputs):
    S, L = 2048, 2
    nc = build(S, L)
    maps = make_in_maps(inputs, list(range(8)), S, L)
    res = run_bass_kernel_spmd(nc, maps, core_ids=list(range(8)))
    return np.stack([np.asarray(r["out"], dtype=np.float32) for r in res.results], axis=0)
```
